# Optimizing a Trainium2 kernel written in Bass

```python
import math
import jax, jax.numpy as jnp
from jax import lax
import numpy as np

D_MODEL = 2048
BATCH = 4
SEQ = 4096
DEPTH = 2

POOL_WINDOWS = (2, 4, 8, 16)
N_POOL_GROUPS = len(POOL_WINDOWS)
POOL_GROUP_DIM = D_MODEL // N_POOL_GROUPS
HEAD_DIM = 128
N_HEADS = D_MODEL // HEAD_DIM
DILATED_BRANCHES = ((128, 1), (512, 4), (2048, 16))
ATTN_BLOCK = 128
D_FF = 128 * ((8 * D_MODEL // 3 + 127) // 128)
CONV_WIDTH = 3
N_A_LAYERS = DEPTH // 2
N_B_LAYERS = DEPTH - N_A_LAYERS
DEEPNORM_ALPHA = (2.0 * DEPTH) ** 0.25
DEEPNORM_BETA = (8.0 * DEPTH) ** -0.25
LN_EPS = 1e-5
NEG_INF = -1e30

kernel_name = "yoco_pool_dilated_attn_convffn_deepnorm"


def layer_norm(x, g, b):
    xf = x.astype(jnp.float32)
    mu = jnp.mean(xf, axis=-1, keepdims=True)
    var = jnp.mean(jnp.square(xf - mu), axis=-1, keepdims=True)
    y = (xf - mu) * lax.rsqrt(var + LN_EPS) * g.astype(jnp.float32) + b.astype(jnp.float32)
    return y.astype(x.dtype)


def pool_mixer(h, w_in, w_grp, scale, w_out):
    B_, S, D = h.shape
    p = (h @ w_in).reshape(B_, S, N_POOL_GROUPS, POOL_GROUP_DIM).astype(jnp.float32)
    cs = jnp.cumsum(p, axis=1)
    t = jnp.arange(S)
    outs = []
    for g, w in enumerate(POOL_WINDOWS):
        c = cs[:, :, g]
        lag = jnp.pad(c[:, :S - w], ((0, 0), (w, 0), (0, 0)))
        cnt = jnp.minimum(t + 1, w).astype(jnp.float32)[None, :, None]
        outs.append((c - lag) / cnt - p[:, :, g])
    pooled = jnp.stack(outs, axis=2).astype(h.dtype)
    mixed = jnp.einsum('bsgc,gcd->bsgd', pooled, w_grp).reshape(B_, S, D) * scale
    return mixed @ w_out


def _dilated_branch(q, k, v, window, dilation):
    B_, S, H, Dh = q.shape
    L = S // dilation
    wr = window // dilation
    assert wr <= ATTN_BLOCK
    N = B_ * dilation

    def to_res(a):
        return a.reshape(B_, L, dilation, H, Dh).transpose(0, 2, 3, 1, 4).reshape(N, H, L, Dh)

    nb = -(-L // ATTN_BLOCK)
    Lp = nb * ATTN_BLOCK
    qr = jnp.pad(to_res(q), ((0, 0), (0, 0), (0, Lp - L), (0, 0)))
    kr = jnp.pad(to_res(k), ((0, 0), (0, 0), (ATTN_BLOCK, Lp - L), (0, 0)))
    vr = jnp.pad(to_res(v), ((0, 0), (0, 0), (ATTN_BLOCK, Lp - L), (0, 0)))
    qb = qr.reshape(N, H, nb, ATTN_BLOCK, Dh)
    kb = kr.reshape(N, H, nb + 1, ATTN_BLOCK, Dh)
    vb = vr.reshape(N, H, nb + 1, ATTN_BLOCK, Dh)
    kw = jnp.concatenate([kb[:, :, :-1], kb[:, :, 1:]], axis=3)
    vw = jnp.concatenate([vb[:, :, :-1], vb[:, :, 1:]], axis=3)

    s = jnp.einsum('nhbqd,nhbkd->nhbqk', qb, kw).astype(jnp.float32) * (1.0 / math.sqrt(Dh))
    qi = jnp.arange(ATTN_BLOCK)[:, None]
    kj = jnp.arange(2 * ATTN_BLOCK)[None, :]
    dist = ATTN_BLOCK + qi - kj
    band = (dist >= 0) & (dist <= wr)
    key_pos = jnp.arange(nb)[:, None, None] * ATTN_BLOCK - ATTN_BLOCK + kj[None]
    mask = band[None] & (key_pos >= 0)
    s = jnp.where(mask, s, NEG_INF)
    m = jnp.max(s, axis=-1, keepdims=True)
    pr = jnp.exp(s - m)
    den = jnp.sum(pr, axis=-1)
    o = jnp.einsum('nhbqk,nhbkd->nhbqd', pr, vw.astype(jnp.float32)) / den[..., None]
    lse = m[..., 0] + jnp.log(den)

    o = o.reshape(N, H, Lp, Dh)[:, :, :L].reshape(B_, dilation, H, L, Dh)
    o = o.transpose(0, 3, 1, 2, 4).reshape(B_, S, H, Dh)
    lse = lse.reshape(N, H, Lp)[:, :, :L].reshape(B_, dilation, H, L)
    lse = lse.transpose(0, 3, 1, 2).reshape(B_, S, H)
    return o, lse


def dilated_attention(h, k, v, w_q, w_o):
    B_, S, _ = h.shape
    q = (h @ w_q).reshape(B_, S, N_HEADS, HEAD_DIM)
    outs, lses = [], []
    for window, dil in DILATED_BRANCHES:
        o, lse = _dilated_branch(q, k, v, window, dil)
        outs.append(o)
        lses.append(lse)
    wts = jax.nn.softmax(jnp.stack(lses, axis=0), axis=0)
    o = jnp.sum(wts[..., None] * jnp.stack(outs, axis=0), axis=0)
    return o.reshape(B_, S, D_MODEL).astype(h.dtype) @ w_o


def conv_ffn(h, w_up, conv_w, conv_b, w_down):
    S = h.shape[1]
    u = h @ w_up
    up = jnp.pad(u, ((0, 0), (CONV_WIDTH - 1, 0), (0, 0)))
    c = conv_b + up[:, 0:S] * conv_w[0]
    for j in range(1, CONV_WIDTH):
        c = c + up[:, j:j + S] * conv_w[j]
    gate, val = jnp.split(c, 2, axis=-1)
    return (jax.nn.silu(gate) * val) @ w_down


def setup_inputs(seed: int = 0) -> dict:
    key = jax.random.key(seed)
    ks = jax.random.split(key, 20)
    f32 = jnp.float32
    D, G, C, F = D_MODEL, N_POOL_GROUPS, POOL_GROUP_DIM, D_FF

    def nrm(k, shape, scale):
        return jax.random.normal(k, shape, f32) * scale

    return {
        "x": nrm(ks[0], (BATCH, SEQ, D), 1.0),
        "pool_w_in": nrm(ks[1], (N_A_LAYERS, D, D), D ** -0.5),
        "pool_w_grp": nrm(ks[2], (N_A_LAYERS, G, C, C), C ** -0.5),
        "pool_scale": 1.0 + nrm(ks[3], (N_A_LAYERS, D), 0.1),
        "pool_w_out": nrm(ks[4], (N_A_LAYERS, D, D), D ** -0.5 * DEEPNORM_BETA),
        "attn_w_q": nrm(ks[5], (N_B_LAYERS, D, D), D ** -0.5),
        "attn_w_o": nrm(ks[6], (N_B_LAYERS, D, D), D ** -0.5 * DEEPNORM_BETA),
        "shared_w_k": nrm(ks[7], (D, D), D ** -0.5),
        "shared_w_v": nrm(ks[8], (D, D), D ** -0.5 * DEEPNORM_BETA),
        "ffn_w_up": nrm(ks[9], (DEPTH, D, 2 * F), D ** -0.5 * DEEPNORM_BETA),
        "ffn_conv_w": nrm(ks[10], (DEPTH, CONV_WIDTH, 2 * F), CONV_WIDTH ** -0.5),
        "ffn_conv_b": nrm(ks[11], (DEPTH, 2 * F), 0.02),
        "ffn_w_down": nrm(ks[12], (DEPTH, F, D), F ** -0.5 * DEEPNORM_BETA),
        "ln1_g": 1.0 + nrm(ks[13], (DEPTH, D), 0.02),
        "ln1_b": nrm(ks[14], (DEPTH, D), 0.02),
        "ln2_g": 1.0 + nrm(ks[15], (DEPTH, D), 0.02),
        "ln2_b": nrm(ks[16], (DEPTH, D), 0.02),
    }


def reference(x, pool_w_in, pool_w_grp, pool_scale, pool_w_out, attn_w_q, attn_w_o,
              shared_w_k, shared_w_v, ffn_w_up, ffn_conv_w, ffn_conv_b, ffn_w_down,
              ln1_g, ln1_b, ln2_g, ln2_b):
    B_, S, _ = x.shape
    h = x
    k_shared = None
    v_shared = None
    for i in range(DEPTH):
        if i < N_A_LAYERS:
            mix = pool_mixer(h, pool_w_in[i], pool_w_grp[i], pool_scale[i], pool_w_out[i])
        else:
            if i == N_A_LAYERS:
                k_shared = (h @ shared_w_k).reshape(B_, S, N_HEADS, HEAD_DIM)
                v_shared = (h @ shared_w_v).reshape(B_, S, N_HEADS, HEAD_DIM)
            j = i - N_A_LAYERS
            mix = dilated_attention(h, k_shared, v_shared, attn_w_q[j], attn_w_o[j])
        h = layer_norm(DEEPNORM_ALPHA * h + mix, ln1_g[i], ln1_b[i])
        ff = conv_ffn(h, ffn_w_up[i], ffn_conv_w[i], ffn_conv_b[i], ffn_w_down[i])
        h = layer_norm(DEEPNORM_ALPHA * h + ff, ln2_g[i], ln2_b[i])
    return h
```

```python
import numpy as np
import ml_dtypes
from contextlib import ExitStack
import concourse.bass as bass
import concourse.mybir as mybir
from concourse.bass_utils import run_bass_kernel_spmd

F32 = mybir.dt.float32
BF16 = mybir.dt.bfloat16
AF = mybir.ActivationFunctionType
ALU = mybir.AluOpType

D = 2048
KC = 16
FF = 5504
JC = 43
T = 512
NPASS = 4
TOK = 2048
HL = 32
ALPHA = 4.0 ** 0.25
EPS = 1e-5
NCORES = 8
import os
NOCC = bool(os.environ.get('YK_NOCC'))
CCMASK = int(os.environ.get('YK_CCMASK', '15'))
HALFW = bool(os.environ.get('YK_HALFW'))
FUSED = True
SCALE = 1.0 / float(np.sqrt(128.0))
V_SCALE, V_G1, V_B1, V_G2, V_B2, V_CW0, V_CW1, V_CW2, V_CB, NV = 0, 16, 32, 48, 64, 80, 166, 252, 338, 424


class Buf:
    __slots__ = ("name", "last_write", "reads", "sem", "semval")

    def __init__(self, name):
        self.name = name
        self.last_write = None
        self.reads = {}
        self.sem = None
        self.semval = 0


class Eng:
    def __init__(self, name, handle, sem, is_pe=False):
        self.name = name
        self.h = handle
        self.sem = sem
        self.count = 0
        self.ops = []
        self.waited = {}
        self.is_pe = is_pe


class Prog:
    def __init__(self, nc, stack):
        self.nc = nc
        self.stack = stack
        self.nsem = 0
        self.phase = 0
        self.pstack = ExitStack()
        self.dma_bufs = []
        self.free_sems = []
        self.cc_chain = Buf("cc_chain")
        self.pe = Eng("pe", nc.tensor, self.new_sem("s_pe"), is_pe=True)
        self.act = Eng("act", nc.scalar, self.new_sem("s_act"))
        self.dve = Eng("dve", nc.vector, self.new_sem("s_dve"))
        self.pool = Eng("pool", nc.gpsimd, self.new_sem("s_pool"))
        self.sp = Eng("sp", nc.sync, self.new_sem("s_sp"))

    def new_sem(self, name):
        self.nsem += 1
        return self.stack.enter_context(self.nc.semaphore(f"{name}_{self.nsem}"))

    def buf_sem(self, b, prefix):
        if self.free_sems:
            b.sem, b.semval = self.free_sems.pop()
        else:
            b.sem = self.new_sem(prefix + b.name)
        self.dma_bufs.append(b)

    def release(self, bufs):
        dead = set(id(b) for b in bufs)
        keep = []
        for b in self.dma_bufs:
            if id(b) in dead:
                self.free_sems.append((b.sem, b.semval))
            else:
                keep.append(b)
        self.dma_bufs = keep

    def sbuf(self, name, shape, dt, persistent=False):
        if persistent:
            return self.stack.enter_context(self.nc.sbuf_tensor("sb_" + name, shape, dt))
        return self.pstack.enter_context(self.nc.sbuf_tensor(f"sb{self.phase}_" + name, shape, dt))

    def psum(self, name, shape, dt=F32):
        return self.stack.enter_context(self.nc.psum_tensor("ps_" + name, shape, dt))

    def _deps(self, eng, reads, writes, parallel=False):
        deps = []
        for b in reads:
            if b.last_write is not None:
                deps.append(b.last_write)
        for b in writes:
            if b.last_write is not None and not parallel:
                deps.append(b.last_write)
            deps.extend(b.reads.values())
        for (sem, val) in deps:
            if eng.is_pe and sem is eng.sem:
                continue
            key = id(sem)
            if eng.waited.get(key, 0) < val:
                eng.waited[key] = val
                eng.ops.append(("wait", sem, val))

    def _record(self, ev, reads, writes):
        for b in writes:
            b.last_write = ev
            b.reads = {}
        for b in reads:
            if b in writes:
                continue
            b.reads[id(ev[0])] = ev

    def op(self, eng, fn, reads=(), writes=(), signal=True):
        self._deps(eng, reads, writes)
        if signal:
            eng.count += 1
            ev = (eng.sem, eng.count)
            eng.ops.append(("op_inc", fn, eng.sem))
        else:
            ev = (eng.sem, eng.count + 1)
            eng.ops.append(("op", fn))
        self._record(ev, reads, writes)
        return ev

    def dma(self, eng, out_ap, in_ap, dsts, srcs, parallel=False):
        if isinstance(dsts, Buf):
            dsts = [dsts]
        if isinstance(srcs, Buf):
            srcs = [srcs]
        self._deps(eng, srcs, dsts, parallel)
        d0 = dsts[0]
        if d0.sem is None:
            self.buf_sem(d0, "d_")
        d0.semval += 16
        ev = (d0.sem, d0.semval)
        if HALFW and eng is self.pool and len(out_ap.shape) == 3:
            h = out_ap.shape[1] // 2
            out_ap, in_ap = out_ap[:, 0:h, :], in_ap[:, 0:h, :]
        eng.ops.append(("dma", out_ap, in_ap, d0.sem))
        self._record(ev, srcs, dsts)
        return ev

    def cc(self, in_ap, out_ap, dst, src, groups, kind=0):
        eng = self.pool
        if NOCC or not (CCMASK >> kind) & 1:
            n = in_ap.shape[0]
            return self.dma(self.sp, out_ap[0:n], in_ap, dst, src)
        self._deps(eng, [src, self.cc_chain], [dst], True)
        if dst.sem is None:
            dst.sem = self.new_sem("c_" + dst.name)
            self.dma_bufs.append(dst)
        dst.semval += 1
        ev = (dst.sem, dst.semval)
        eng.ops.append(("cc", in_ap, out_ap, dst.sem, groups))
        self._record(ev, [src], [dst, self.cc_chain])
        return ev

    def barrier(self):
        engs = [self.pe, self.act, self.dve, self.pool, self.sp]
        for e in engs:
            for o in (self.pe, self.act, self.dve, self.pool):
                if o.count > 0 and e.waited.get(id(o.sem), 0) < o.count:
                    e.waited[id(o.sem)] = o.count
                    e.ops.append(("wait", o.sem, o.count))
            for b in self.dma_bufs:
                if e.waited.get(id(b.sem), 0) < b.semval:
                    e.waited[id(b.sem)] = b.semval
                    e.ops.append(("wait", b.sem, b.semval))

    def end_phase(self):
        self.barrier()
        self.emit()
        for e in (self.pe, self.act, self.dve, self.pool, self.sp):
            e.ops = []
        self.pstack.close()
        self.pstack = ExitStack()
        self.phase += 1

    def wait_all(self, eng, bufs):
        self._deps(eng, bufs, [])

    def emit(self):
        with self.nc.Block() as block:
            def run(e):
                def body(h):
                    for o in e.ops:
                        k = o[0]
                        if k == "wait":
                            h.wait_ge(o[1], o[2])
                        elif k == "op_inc":
                            o[1](h).then_inc(o[2], 1)
                        elif k == "op":
                            o[1](h)
                        elif k == "cc":
                            h.collective_compute("AllGather", ALU.bypass, replica_groups=o[4], ins=[o[1].opt()], outs=[o[2].opt()]).then_inc(o[3], 1)
                        else:
                            h.dma_start(out=o[1], in_=o[2]).then_inc(o[3], 16)
                return body
            block.tensor(run(self.pe))
            block.scalar(run(self.act))
            block.vector(run(self.dve))
            block.gpsimd(run(self.pool))
            block.sync(run(self.sp))


def ss(start, n, step):
    return slice(start, start + step * (n - 1) + 1, step)


class Ctx:
    def __init__(self):
        self.nc = bass.Bass("TRN2", target_bir_lowering=False)
        self.st = ExitStack()
        self.P = Prog(self.nc, self.st)
        self.bufs = {}
        self.outs = []
        P = self.P
        self.slots = []
        for i in range(3):
            t = P.sbuf(f"slot{i}", [128, 4096], BF16, persistent=True)
            self.slots.append((t, Buf(f"slot{i}a"), Buf(f"slot{i}b")))
        self.banks = []
        for i in range(8):
            self.banks.append((P.psum(f"bank{i}", [128, 512]), Buf(f"bank{i}")))
        self.main_i = 0
        self.Bw = Buf("weights")
        self.ones32 = P.sbuf("ones32", [128, 128], F32, persistent=True)
        self.epst = P.sbuf("epst", [128, 1], F32, persistent=True)
        self.Bconst = Buf("const")
        P.op(P.dve, lambda h: h.memset(self.ones32[:], 1.0 / D), writes=[self.Bconst])
        P.op(P.dve, lambda h: h.memset(self.epst[:], EPS), writes=[self.Bconst])
        self.vecs = P.sbuf("vecs", [128, NV], F32, persistent=True)
        self.Bvecs = Buf("vecs")
        self.hvt = P.sbuf("hvt", [128, 1], F32, persistent=True)
        self.Bhv = Buf("hv")

    def B(self, name):
        b = self.bufs.get(name)
        if b is None:
            b = self.bufs[name] = Buf(name)
        return b

    def din(self, name, shape, dt):
        return self.nc.dram_tensor(name, list(shape), dt, kind="ExternalInput").ap()

    def dout(self, name, shape, dt):
        ap = self.nc.dram_tensor(name, list(shape), dt, kind="ExternalOutput").ap()
        self.outs.append(self.B("dram_" + name))
        return ap

    def main_bank(self):
        b = self.banks[self.main_i % 2]
        self.main_i += 1
        return b

    def alloc_ln(self):
        P = self.P
        self.acc1 = P.sbuf("acc1", [128, T], F32)
        self.acc2 = P.sbuf("acc2", [128, T], F32)
        self.sqt = [P.sbuf(f"sqt{i}", [128, T], F32) for i in range(2)]
        self.mean = P.sbuf("mean", [128, T], F32)
        self.rstd = P.sbuf("rstd", [128, T], F32)
        self.m2 = P.sbuf("m2", [128, T], F32)
        self.sq_i = 0

    def mm(self, ps_ap, psbuf, lhs_fn, rhs_fn, nk, reads):
        P = self.P
        for k in range(nk):
            l, r = lhs_fn(k), rhs_fn(k)
            P.op(P.pe, (lambda h, k=k, l=l, r=r: h.matmul(ps_ap, l, r, start=(k == 0), stop=(k == nk - 1))),
                 reads=reads, writes=[psbuf], signal=(k == nk - 1))

    def resid_acc(self, oc, ps, psb, h32, h32b):
        P = self.P
        yv = h32[:, oc, :]
        P.op(P.dve, lambda h: h.scalar_tensor_tensor(yv, yv, ALPHA, ps[:, 0:T], ALU.mult, ALU.add),
             reads=[psb], writes=[h32b[oc]])
        a1, a2 = self.acc1, self.acc2
        B1, B2 = self.B("acc1"), self.B("acc2")
        if oc == 0:
            P.op(P.pool, lambda h: h.tensor_copy(a1[:], yv), reads=[h32b[oc]], writes=[B1])
            P.op(P.pool, lambda h: h.tensor_tensor(a2[:], yv, yv, ALU.mult), reads=[h32b[oc]], writes=[B2])
        else:
            sq = self.sqt[self.sq_i % 2]
            sqb = self.B(f"sqt{self.sq_i % 2}")
            self.sq_i += 1
            P.op(P.pool, lambda h: h.tensor_tensor(a1[:], a1[:], yv, ALU.add), reads=[h32b[oc]], writes=[B1])
            P.op(P.pool, lambda h: h.tensor_tensor(sq[:], yv, yv, ALU.mult), reads=[h32b[oc]], writes=[sqb])
            P.op(P.pool, lambda h: h.tensor_tensor(a2[:], a2[:], sq[:], ALU.add), reads=[sqb], writes=[B2])

    def ln_finalize(self, h32, h32b, hbo, hbob, gcol, bcol):
        P = self.P
        B1, B2 = self.B("acc1"), self.B("acc2")
        Bm, Br, Bm2 = self.B("mean"), self.B("rstd"), self.B("m2")
        ps1, pb1 = self.main_bank()
        ps2, pb2 = self.main_bank()
        P.op(P.pe, lambda h: h.matmul(ps1[:], self.ones32[:], self.acc1[:], start=True, stop=True),
             reads=[B1, self.Bconst], writes=[pb1])
        P.op(P.pe, lambda h: h.matmul(ps2[:], self.ones32[:], self.acc2[:], start=True, stop=True),
             reads=[B2, self.Bconst], writes=[pb2])
        P.op(P.act, lambda h: h.activation(self.mean[:], ps1[:], AF.Identity), reads=[pb1], writes=[Bm])
        P.op(P.dve, lambda h: h.tensor_tensor(self.m2[:], self.mean[:], self.mean[:], ALU.mult), reads=[Bm], writes=[Bm2])
        P.op(P.dve, lambda h: h.tensor_tensor(self.m2[:], ps2[:], self.m2[:], ALU.subtract), reads=[pb2, Bm2], writes=[Bm2])
        P.op(P.act, lambda h: h.activation(self.rstd[:], self.m2[:], AF.Sqrt, bias=self.epst[:, 0:1], scale=1.0),
             reads=[Bm2, self.Bconst], writes=[Br])
        P.op(P.dve, lambda h: h.reciprocal(self.rstd[:], self.rstd[:]), reads=[Br], writes=[Br])
        for oc in range(KC):
            yv = h32[:, oc, :]
            P.op(P.dve, lambda h, yv=yv: h.tensor_tensor(yv, yv, self.mean[:], ALU.subtract), reads=[Bm], writes=[h32b[oc]])
            P.op(P.dve, lambda h, yv=yv: h.tensor_tensor(yv, yv, self.rstd[:], ALU.mult), reads=[Br], writes=[h32b[oc]])
            P.op(P.act, lambda h, yv=yv, oc=oc: h.activation(yv, yv, AF.Identity, bias=self.vecs[:, bcol + oc:bcol + oc + 1],
                                                             scale=self.vecs[:, gcol + oc:gcol + oc + 1]),
                 reads=[self.Bvecs], writes=[h32b[oc]])
            P.op(P.pool, lambda h, yv=yv, oc=oc: h.tensor_copy(hbo[:, oc, :], yv), reads=[h32b[oc]], writes=[hbob[oc]])

    def run_tasks(self, tasks):
        P = self.P
        ring_s = [i for i, t in enumerate(tasks) if t[0] in ("s128", "sgrp", "sv", "sup")]
        ring_d = [i for i, t in enumerate(tasks) if t[0] == "d"]
        pos_s = {t: k for k, t in enumerate(ring_s)}
        pos_d = {t: k for k, t in enumerate(ring_d)}
        views = {}
        loads = ring_s + ring_d
        loads.sort()
        nl = 0

        def can_load(t, i):
            if t > i + 8:
                return False
            if t in pos_s:
                k = pos_s[t]
                return k < 3 or ring_s[k - 3] < i
            k = pos_d[t]
            return k < 2 or ring_d[k - 2] < i

        def do_load(t):
            kind, src, _ = tasks[t]
            if kind == "d":
                k = pos_d[t] % 2
                wt, wb = self.wd[k]
                v = wt[:].rearrange("p (j n) -> p j n", n=128)
                P.dma(P.pool, v, src.rearrange("(j p) n -> p j n", p=128), wb, self.Bw)
                views[t] = (v, [wb])
                return
            k = pos_s[t] % 3
            st, ba, bb = self.slots[k]
            if kind == "s128":
                v = st[:, 0:2048].rearrange("p (k n) -> p k n", n=128)
                P.dma(P.pool, v, src.rearrange("(k p) n -> p k n", p=128), [ba, bb], self.Bw)
                views[t] = (v, [ba])
            elif kind == "sgrp":
                v = st[:, 0:2048].rearrange("p (k n) -> p k n", n=512)
                P.dma(P.pool, v, src.rearrange("(k p) n -> p k n", p=128), [ba, bb], self.Bw)
                views[t] = (v, [ba])
            elif kind == "sv":
                v = st[:, 0:4096].rearrange("p (k n) -> p k n", n=256)
                P.dma(P.pool, v, src.rearrange("(k p) n -> p k n", p=128), [ba, bb], self.Bw)
                views[t] = (v, [ba, bb])
            elif kind == "sup":
                v0 = st[:, 0:2048].rearrange("p (k n) -> p k n", n=128)
                v1 = st[:, 2048:4096].rearrange("p (k n) -> p k n", n=128)
                P.dma(P.pool, v0, src[0].rearrange("(k p) n -> p k n", p=128), ba, self.Bw)
                P.dma(P.pool, v1, src[1].rearrange("(k p) n -> p k n", p=128), bb, self.Bw)
                views[t] = ((v0, v1), [ba, bb])

        for i, (kind, src, fn) in enumerate(tasks):
            while nl < len(loads) and can_load(loads[nl], i):
                do_load(loads[nl])
                nl += 1
            if kind == "none":
                fn(None, None)
            else:
                v, wb = views.pop(i)
                fn(v, wb)

    def end_phase(self):
        self.P.end_phase()
        self.P.release([v for k, v in self.bufs.items() if not k.startswith("dram_")])
        self.bufs = {k: v for k, v in self.bufs.items() if k.startswith("dram_")}

    def finish(self):
        self.P.wait_all(self.P.sp, self.outs)
        self.P.barrier()
        self.P.emit()
        self.P.pstack.close()
        self.st.close()
        return self.nc


def load_vecs(c, vec_ap):
    c.P.dma(c.P.sp, c.vecs[:], vec_ap, c.Bvecs, c.B("dram_vecs_in"))


def phase_A1(c, io):
    P = c.P
    xT, w_in, w_grp, w_out, invc_d = io["xT"], io["w_in"], io["w_grp"], io["w_out"], io["invc"]
    Hm32, HmbH, halo_s = io["Hm32"], io["HmbH"], io["halo0s"]
    invc = P.sbuf("invc", [128, 4, 16], F32)
    P.dma(P.sp, invc[:], invc_d.rearrange("p (g t) -> p g t", t=16), c.B("invc"), c.B("dram_invc"))
    c.alloc_ln()
    h32 = P.sbuf("h32", [128, KC, T], F32)
    hb = P.sbuf("hb", [128, KC, HL + T], BF16)
    hbo = P.sbuf("hbo", [128, KC, T], BF16)
    pooled = P.sbuf("pooled", [128, KC, T], BF16)
    mixed = P.sbuf("mixed", [128, KC, T], BF16)
    pcarry = P.sbuf("pcarry", [128, KC, HL], F32)
    psb = [P.sbuf(f"psb{i}", [128, HL + T], F32) for i in range(2)]
    tA = [P.sbuf(f"tA{i}", [128, HL + T], F32) for i in range(2)]
    tB = [P.sbuf(f"tB{i}", [128, HL + T], F32) for i in range(2)]
    tmpc = P.sbuf("tmpc", [128, 16], F32)
    h32b = [c.B(f"h32_{i}") for i in range(KC)]
    hbb = [c.B(f"hb_{i}") for i in range(KC)]
    hbob = [c.B(f"hbo_{i}") for i in range(KC)]
    poolb = [c.B(f"pooled_{i}") for i in range(KC)]
    mixb = [c.B(f"mixed_{i}") for i in range(KC)]
    pcb = [c.B(f"pcarry_{i}") for i in range(KC)]
    Bx = c.B("dram_x")
    xv = xT.rearrange("(c p) t -> p c t", p=128)
    tasks = []

    def load_x(q):
        def f(v, wb):
            P.dma(P.sp, h32[:], xv[:, :, HL + T * q: HL + T * q + T], h32b, Bx)
            P.dma(P.pool, hb[:], xv[:, :, T * q: T * q + HL + T], hbb, Bx)
        return f

    def pproj(q, oc):
        def f(W, wb):
            s = oc % 2
            p_sb, pb_ = psb[s], c.B(f"psb{s}")
            ta, tab, tb, tbb = tA[s], c.B(f"tA{s}"), tB[s], c.B(f"tB{s}")
            ps, pbk = c.main_bank()
            c.mm(ps[:, 0:T], pbk, lambda k: W[:, k, :], lambda k: hb[:, k, HL:HL + T], KC, wb + hbb)
            if q == 0:
                psh, pbh = c.banks[2]
                c.mm(psh[:, 0:HL], pbh, lambda k: W[:, k, :], lambda k: hb[:, k, 0:HL], KC, wb + hbb)
                P.op(P.act, lambda h: h.activation(p_sb[:, 0:HL], psh[:, 0:HL], AF.Identity), reads=[pbh], writes=[pb_])
            else:
                P.op(P.pool, lambda h: h.tensor_copy(p_sb[:, 0:HL], pcarry[:, oc, :]), reads=[pcb[oc]], writes=[pb_])
            P.op(P.act, lambda h: h.activation(p_sb[:, HL:HL + T], ps[:, 0:T], AF.Identity), reads=[pbk], writes=[pb_])
            P.op(P.pool, lambda h: h.tensor_copy(pcarry[:, oc, :], p_sb[:, T:T + HL]), reads=[pb_], writes=[pcb[oc]])
            g = oc // 4
            w = 2 << g
            cur, curb = p_sb, pb_
            N = HL + T
            for k in range(g + 1):
                sh = 1 << k
                nxt, nxtb = (ta, tab) if k % 2 == 0 else (tb, tbb)
                P.op(P.dve, lambda h, cur=cur, nxt=nxt, sh=sh: h.tensor_tensor(nxt[:, sh:N], cur[:, sh:N], cur[:, 0:N - sh], ALU.add),
                     reads=[curb], writes=[nxtb])
                cur, curb = nxt, nxtb
            P.op(P.dve, lambda h, cur=cur: h.scalar_tensor_tensor(pooled[:, oc, :], cur[:, HL:N], 1.0 / w, p_sb[:, HL:N], ALU.mult, ALU.subtract),
                 reads=[curb, pb_], writes=[poolb[oc]])
            if q == 0:
                Bt = c.B("tmpc")
                P.op(P.dve, lambda h, cur=cur: h.tensor_tensor(tmpc[:], cur[:, HL:HL + 16], invc[:, g, :], ALU.mult),
                     reads=[curb, c.B("invc")], writes=[Bt])
                P.op(P.dve, lambda h: h.tensor_tensor(pooled[:, oc, 0:16], tmpc[:], p_sb[:, HL:HL + 16], ALU.subtract),
                     reads=[Bt, pb_], writes=[poolb[oc]])
        return f

    def grp(q, g):
        def f(W, wb):
            for ocl in range(4):
                oc = 4 * g + ocl
                ps, pbk = c.main_bank()
                c.mm(ps[:, 0:T], pbk, lambda k: W[:, k, 128 * ocl:128 * ocl + 128], lambda k: pooled[:, 4 * g + k, :], 4,
                     wb + poolb[4 * g:4 * g + 4])
                P.op(P.act, lambda h, oc=oc, ps=ps: h.activation(mixed[:, oc, :], ps[:, 0:T], AF.Identity,
                                                                 scale=c.vecs[:, V_SCALE + oc:V_SCALE + oc + 1]),
                     reads=[pbk, c.Bvecs], writes=[mixb[oc]])
        return f

    def wout(q, oc):
        def f(W, wb):
            ps, pbk = c.main_bank()
            c.mm(ps[:, 0:T], pbk, lambda k: W[:, k, :], lambda k: mixed[:, k, :], KC, wb + mixb)
            c.resid_acc(oc, ps, pbk, h32, h32b)
        return f

    def fin(q):
        def f(v, wb):
            c.ln_finalize(h32, h32b, hbo, hbob, V_G1, V_B1)
            P.dma(P.sp, Hm32.rearrange("(c p) t -> p c t", p=128)[:, :, T * q:T * q + T], h32[:], c.B("dram_Hm32"), h32b)
            P.dma(P.sp, HmbH.rearrange("(c p) t -> p c t", p=128)[:, :, 2 + T * q:2 + T * q + T], hbo[:], c.B("dram_HmbH"), hbob)
            if q == NPASS - 1:
                P.dma(P.sp, halo_s.rearrange("p (c t) -> p c t", t=2), hbo[:, :, T - 2:T], c.B("dram_halo0s"), hbob)
        return f

    for q in range(NPASS):
        tasks.append(("none", None, load_x(q)))
        for oc in range(KC):
            tasks.append(("s128", w_in[:, 128 * oc:128 * oc + 128], pproj(q, oc)))
        for g in range(4):
            tasks.append(("sgrp", w_grp[g], grp(q, g)))
        for oc in range(KC):
            tasks.append(("s128", w_out[:, 128 * oc:128 * oc + 128], wout(q, oc)))
        tasks.append(("none", None, fin(q)))
    c.wd = []
    c.run_tasks(tasks)
    c.end_phase()


def phase_FFN(c, io, L, with_qkv):
    P = c.P
    if L == 0:
        Hin32, HinbH, halo_g, Hout = io["Hm32"], io["HmbH"], io["halo0g"], io["H1T"]
        n32, nbH, nhg, nout = "dram_Hm32", "dram_HmbH", "dram_halo0g", "dram_H1T"
    else:
        Hin32, HinbH, halo_g, Hout = io["H2_32"], io["H2bH"], io["halo1g"], io["out"]
        n32, nbH, nhg, nout = "dram_H2_32", "dram_H2bH", "dram_halo1g", "dram_out"
    w_up, w_down = io[f"w_up{L}"], io[f"w_dn{L}"]
    if with_qkv:
        w_q, w_k, w_v = io["w_q"], io["w_k"], io["w_v"]
        QT, KTc, Vc = io["QT"], io["KTc"], io["Vc"]
    c.alloc_ln()
    c.wd = [(P.sbuf(f"wd{i}", [128, JC * 128], BF16), Buf(f"wd{i}")) for i in range(2)]
    h32 = P.sbuf("h32", [128, KC, T], F32)
    hb = P.sbuf("hb", [128, KC, 2 + T], BF16)
    hbo = P.sbuf("hbo", [128, KC, T], BF16)
    big = P.sbuf("big", [128, JC * T], BF16)
    a_all = big[:].rearrange("p (j t) -> p j t", t=T)
    ucarry = P.sbuf("ucarry", [128, 2 * JC, 2], F32)
    u_sb = [[P.sbuf(f"u{s}{gv}", [128, 2 + T], F32) for gv in range(2)] for s in range(2)]
    cb_ = [[P.sbuf(f"c{s}{gv}", [128, T], F32) for gv in range(2)] for s in range(2)]
    sg = [P.sbuf(f"sg{s}", [128, T], F32) for s in range(2)]
    h32b = [c.B(f"h32_{i}") for i in range(KC)]
    hbb = [c.B(f"hb_{i}") for i in range(KC)]
    hbob = [c.B(f"hbo_{i}") for i in range(KC)]
    ab = [c.B(f"a_{j}") for j in range(JC)]
    ucb = [c.B(f"uc_{j}") for j in range(2 * JC)]
    if with_qkv:
        qst = [P.sbuf(f"qst{i}", [128, T], BF16) for i in range(2)]
        vst = big[:, 0:4 * D].rearrange("p (tb f) -> p tb f", f=D)
    tasks = []
    h32v = Hin32.rearrange("(c p) t -> p c t", p=128)
    hbv = HinbH.rearrange("(c p) t -> p c t", p=128)

    def load_h(q):
        def f(v, wb):
            P.dma(P.sp, h32[:], h32v[:, :, T * q:T * q + T], h32b, c.B(n32))
            if q == 0:
                P.dma(P.sp, hb[:, :, 2:2 + T], hbv[:, :, 2:2 + T], hbb, c.B(nbH))
                P.dma(P.sp, hb[:, :, 0:2], halo_g[0:128, :].rearrange("p (c t) -> p c t", t=2), hbb, c.B(nhg))
                P.op(P.dve, lambda h: h.tensor_scalar_mul(hb[:, :, 0:2], hb[:, :, 0:2], c.hvt[:, 0:1]), reads=[c.Bhv], writes=hbb)
            else:
                P.dma(P.sp, hb[:], hbv[:, :, T * q:T * q + 2 + T], hbb, c.B(nbH))
        return f

    def up(q, j):
        def f(Ws, wbs):
            s = j % 2
            for gv in range(2):
                W, wb = Ws[gv], [wbs[gv]]
                idx = gv * JC + j
                ps, pbk = c.banks[2 + 2 * s + gv]
                c.mm(ps[:, 0:T], pbk, lambda k: W[:, k, :], lambda k: hb[:, k, 2:2 + T], KC, wb + hbb)
                u, ub = u_sb[s][gv], c.B(f"u{s}{gv}")
                cc, ccb = cb_[s][gv], c.B(f"c{s}{gv}")
                if q == 0:
                    psh, pbh = c.banks[6 + gv]
                    c.mm(psh[:, 0:2], pbh, lambda k: W[:, k, :], lambda k: hb[:, k, 0:2], KC, wb + hbb)
                    P.op(P.act, lambda h, u=u, psh=psh: h.activation(u[:, 0:2], psh[:, 0:2], AF.Identity), reads=[pbh], writes=[ub])
                else:
                    P.op(P.pool, lambda h, u=u, idx=idx: h.tensor_copy(u[:, 0:2], ucarry[:, idx, :]), reads=[ucb[idx]], writes=[ub])
                P.op(P.act, lambda h, u=u, ps=ps: h.activation(u[:, 2:2 + T], ps[:, 0:T], AF.Identity), reads=[pbk], writes=[ub])
                P.op(P.act, lambda h, cc=cc, ps=ps, idx=idx: h.activation(cc[:], ps[:, 0:T], AF.Identity,
                                                                         bias=c.vecs[:, V_CB + idx:V_CB + idx + 1],
                                                                         scale=c.vecs[:, V_CW2 + idx:V_CW2 + idx + 1]),
                     reads=[pbk, c.Bvecs], writes=[ccb])
                P.op(P.pool, lambda h, u=u, idx=idx: h.tensor_copy(ucarry[:, idx, :], u[:, T:T + 2]), reads=[ub], writes=[ucb[idx]])
                P.op(P.dve, lambda h, u=u, cc=cc, idx=idx: h.scalar_tensor_tensor(cc[:], u[:, 1:1 + T], c.vecs[:, V_CW1 + idx:V_CW1 + idx + 1],
                                                                                 cc[:], ALU.mult, ALU.add),
                     reads=[ub, c.Bvecs], writes=[ccb])
                P.op(P.dve, lambda h, u=u, cc=cc, idx=idx: h.scalar_tensor_tensor(cc[:], u[:, 0:T], c.vecs[:, V_CW0 + idx:V_CW0 + idx + 1],
                                                                                 cc[:], ALU.mult, ALU.add),
                     reads=[ub, c.Bvecs], writes=[ccb])
            sgt, sgb = sg[s], c.B(f"sg{s}")
            P.op(P.act, lambda h: h.activation(sgt[:], cb_[s][0][:], AF.Silu), reads=[c.B(f"c{s}0")], writes=[sgb])
            P.op(P.dve, lambda h: h.tensor_tensor(a_all[:, j, :], sgt[:], cb_[s][1][:], ALU.mult),
                 reads=[sgb, c.B(f"c{s}1")], writes=[ab[j]] + ([c.B("vst")] if (with_qkv and j < 16) else []))
        return f

    def down(q, oc):
        def f(W, wb):
            ps, pbk = c.main_bank()
            c.mm(ps[:, 0:T], pbk, lambda k: W[:, k, :], lambda k: a_all[:, k, :], JC, wb + ab)
            c.resid_acc(oc, ps, pbk, h32, h32b)
        return f

    def fin(q):
        def f(v, wb):
            c.ln_finalize(h32, h32b, hbo, hbob, V_G2, V_B2)
            P.dma(P.sp, Hout.rearrange("(c p) t -> p c t", p=128)[:, :, T * q:T * q + T], h32[:], c.B(nout), h32b)
        return f

    qk_i = [0]

    def qk(q, hc, dst, dname):
        def f(W, wb):
            ps, pbk = c.main_bank()
            c.mm(ps[:, 0:T], pbk, lambda k: W[:, k, :], lambda k: hbo[:, k, :], KC, wb + hbob)
            s = qk_i[0] % 2
            qk_i[0] += 1
            P.op(P.act, lambda h: h.activation(qst[s][:], ps[:, 0:T], AF.Identity), reads=[pbk], writes=[c.B(f"qst{s}")])
            if dname == "QT":
                P.dma(P.sp, dst[hc, :, T * q:T * q + T], qst[s][:], c.B("dram_QT"), c.B(f"qst{s}"))
            else:
                P.dma(P.sp, KTc[hc][:, T * q:T * q + T], qst[s][:], c.B("dram_KTc"), c.B(f"qst{s}"), parallel=True)
        return f

    def vproj(q, nt):
        def f(W, wb):
            for tb in range(4):
                ps, pbk = c.main_bank()
                c.mm(ps[:, 0:256], pbk, lambda k: hbo[:, k, 128 * tb:128 * tb + 128], lambda k: W[:, k, :], KC, wb + hbob)
                eng = P.act if tb % 2 == 0 else P.dve
                if tb % 2 == 0:
                    P.op(P.act, lambda h, tb=tb, ps=ps: h.activation(vst[:, tb, 256 * nt:256 * nt + 256], ps[:, 0:256], AF.Identity),
                         reads=[pbk], writes=[c.B("vst")] + ab[0:16])
                else:
                    P.op(P.dve, lambda h, tb=tb, ps=ps: h.tensor_copy(vst[:, tb, 256 * nt:256 * nt + 256], ps[:, 0:256]),
                         reads=[pbk], writes=[c.B("vst")] + ab[0:16])
            if nt == 7:
                for tb in range(4):
                    P.dma(P.sp, Vc[4 * q + tb], vst[:, tb, :], c.B("dram_Vc"), c.B("vst"), parallel=True)
        return f

    for q in range(NPASS):
        tasks.append(("none", None, load_h(q)))
        for j in range(JC):
            tasks.append(("sup", (w_up[:, 128 * j:128 * j + 128], w_up[:, FF + 128 * j:FF + 128 * j + 128]), up(q, j)))
        for oc in range(KC):
            tasks.append(("d", w_down[:, 128 * oc:128 * oc + 128], down(q, oc)))
        tasks.append(("none", None, fin(q)))
        if with_qkv:
            for hc in range(KC):
                tasks.append(("s128", w_q[:, 128 * hc:128 * hc + 128], qk(q, hc, QT, "QT")))
            for hc in range(KC):
                tasks.append(("s128", w_k[:, 128 * hc:128 * hc + 128], qk(q, hc, None, "KT")))
            for nt in range(8):
                tasks.append(("sv", w_v[:, 256 * nt:256 * nt + 256], vproj(q, nt)))
    c.run_tasks(tasks)
    if L == 0:
        c.end_phase()


def phase_B(c, io):
    P = c.P
    QT, KTc, KTg, V2, H1T, w_o, masks_d, OT = io["QT"], io["KTc"], io["KTg"], io["V2"], io["H1T"], io["w_o"], io["masks"], io["OT"]
    H2_32, H2bH, halo_s = io["H2_32"], io["H2bH"], io["halo1s"]
    c.alloc_ln()
    c.wd = []
    masks = P.sbuf("masks", [128, 4, 512], BF16)
    Bmask = c.B("masks")
    P.dma(P.sp, masks[:], masks_d.rearrange("p (m t) -> p m t", t=512), Bmask, c.B("dram_masks"))
    onesb = P.sbuf("onesb", [128, 128], BF16)
    P.op(P.dve, lambda h: h.memset(onesb[:], 1.0), writes=[c.Bconst])
    M_NORM, M_FIRST1, M_ALL0, M_CUR = 0, 1, 2, 3
    qh = [P.sbuf(f"qh{i}", [128, TOK], BF16) for i in range(2)]
    kh = [P.sbuf(f"kh{i}", [128, 2 * TOK], BF16) for i in range(2)]
    vd1 = [P.sbuf(f"vd1_{i}", [128, 17, 128], BF16) for i in range(2)]
    vd4 = [P.sbuf(f"vd4_{i}", [128, 4, 8, 128], BF16) for i in range(2)]
    vd16 = [P.sbuf(f"vd16_{i}", [128, 16, 2, 128], BF16) for i in range(2)]
    numer = P.sbuf("numer", [128, TOK], F32)
    den = P.sbuf("den", [128, TOK], F32)
    obf = P.sbuf("obf", [128, TOK], BF16)
    ptp = [P.sbuf(f"ptp{i}", [128, 512], BF16) for i in range(2)]
    ptc = [P.sbuf(f"ptc{i}", [128, 512], BF16) for i in range(2)]
    BV2 = [c.B("dram_V2")]

    def head_load(h):
        s = h % 2
        P.dma(P.sp, qh[s][:], QT[h], c.B(f"qh{s}"), c.B("dram_QT"))
        P.dma(P.sp, kh[s][:, 0:TOK], KTg[h][0:128, :], c.B(f"kh{s}a"), c.B("dram_KTg"))
        P.dma(P.sp, kh[s][:, TOK:2 * TOK], KTc[h], c.B(f"kh{s}b"), c.B("dram_KTc"))
        hs = slice(128 * h, 128 * h + 128)
        P.dma(P.sp, vd1[s][:], V2[15 * 128:, hs].rearrange("(kb p) d -> p kb d", p=128), c.B(f"vd1_{s}"), BV2)
        for r in range(4):
            P.dma(P.sp, vd4[s][:, r], V2[:, hs].rearrange("(kb p r) d -> r p kb d", p=128, r=4)[r], c.B(f"vd4_{s}"), BV2, parallel=True)
        for r in range(16):
            P.dma(P.sp, vd16[s][:, r], V2[:, hs].rearrange("(kb p r) d -> r p kb d", p=128, r=16)[r], c.B(f"vd16_{s}"), BV2, parallel=True)

    def g_qk(g):
        s, gi, blocks, pmask = g["s"], g["gi"], g["blocks"], g["pmask"]
        SPp, SPb = c.banks[0 + 2 * gi]
        SCp, SCb = c.banks[1 + 2 * gi]
        qb_, kb_, kb2_ = c.B(f"qh{s}"), c.B(f"kh{s}a"), c.B(f"kh{s}b")
        for bi, (qsl, kc_sl, kp_sl, Vc, Vp, vb) in enumerate(blocks):
            o = slice(128 * bi, 128 * bi + 128)
            P.op(P.pe, lambda h_, o=o, kp_sl=kp_sl, qsl=qsl: h_.matmul(SPp[:, o], kh[s][:, kp_sl], qh[s][:, qsl], start=True, stop=True),
                 reads=[qb_, kb_, kb2_], writes=[SPb], signal=False)
            P.op(P.pe, lambda h_, o=o, kc_sl=kc_sl, qsl=qsl: h_.matmul(SCp[:, o], kh[s][:, kc_sl], qh[s][:, qsl], start=True, stop=True),
                 reads=[qb_, kb_, kb2_], writes=[SCb], signal=(bi == 3))
        pp, ppb = ptp[gi], c.B(f"ptp{gi}")
        pc, pcb_ = ptc[gi], c.B(f"ptc{gi}")
        P.op(P.act, lambda h_: h_.activation(pp[:], SPp[:], AF.Exp, scale=SCALE), reads=[SPb], writes=[ppb])
        P.op(P.act, lambda h_: h_.activation(pc[:], SCp[:], AF.Exp, scale=SCALE), reads=[SCb], writes=[pcb_])
        P.op(P.dve, lambda h_: h_.tensor_tensor(pp[:], pp[:], masks[:, pmask, :], ALU.mult), reads=[Bmask], writes=[ppb])
        P.op(P.pool, lambda h_: h_.tensor_tensor(pc[:], pc[:], masks[:, M_CUR, :], ALU.mult), reads=[Bmask], writes=[pcb_])

    def g_pv(g):
        gi, blocks, out_n, out_d, first = g["gi"], g["blocks"], g["on"], g["od"], g["first"]
        POp, POb = c.banks[4 + 2 * gi]
        PDp, PDb = c.banks[5 + 2 * gi]
        pp, ppb = ptp[gi], c.B(f"ptp{gi}")
        pc, pcb_ = ptc[gi], c.B(f"ptc{gi}")
        for bi, (qsl, kc_sl, kp_sl, Vc, Vp, vb) in enumerate(blocks):
            o = slice(128 * bi, 128 * bi + 128)
            P.op(P.pe, lambda h_, o=o, Vp=Vp: h_.matmul(POp[:, o], Vp, pp[:, o], start=True, stop=False),
                 reads=[ppb] + vb, writes=[POb], signal=False)
            P.op(P.pe, lambda h_, o=o, Vc=Vc: h_.matmul(POp[:, o], Vc, pc[:, o], start=False, stop=True),
                 reads=[pcb_] + vb, writes=[POb], signal=False)
        P.op(P.pe, lambda h_: h_.matmul(PDp[:], onesb[:], pp[:], start=True, stop=False),
             reads=[ppb, c.Bconst], writes=[PDb], signal=False)
        P.op(P.pe, lambda h_: h_.matmul(PDp[:], onesb[:], pc[:], start=False, stop=True),
             reads=[pcb_, c.Bconst], writes=[POb, PDb], signal=True)
        Bn, Bd = c.B("numer"), c.B("den")
        pov = POp[:].rearrange("p (r i) -> p r i", i=128)
        pdv = PDp[:].rearrange("p (r i) -> p r i", i=128)
        if first:
            P.op(P.dve, lambda h_: h_.tensor_copy(out_n, pov), reads=[POb], writes=[Bn])
            P.op(P.dve, lambda h_: h_.tensor_copy(out_d, pdv), reads=[PDb], writes=[Bd])
        else:
            P.op(P.dve, lambda h_: h_.tensor_tensor(out_n, out_n, pov, ALU.add), reads=[POb], writes=[Bn])
            P.op(P.dve, lambda h_: h_.tensor_tensor(out_d, out_d, pdv, ALU.add), reads=[PDb], writes=[Bd])

    def head_groups(h):
        s = h % 2
        gl = []
        for a in range(4):
            blocks = []
            for bi in range(4):
                qb = 4 * a + bi
                qsl = slice(128 * qb, 128 * qb + 128)
                kc_sl = slice(TOK + 128 * qb, TOK + 128 * qb + 128)
                kp_sl = slice(TOK + 128 * qb - 128, TOK + 128 * qb)
                blocks.append((qsl, kc_sl, kp_sl, vd1[s][:, qb + 1, :], vd1[s][:, qb, :], [c.B(f"vd1_{s}")]))
            on = numer[:, 512 * a:512 * a + 512].rearrange("p (r i) -> p r i", i=128)
            od = den[:, 512 * a:512 * a + 512].rearrange("p (r i) -> p r i", i=128)
            gl.append(dict(blocks=blocks, pmask=M_FIRST1 if a == 0 else M_NORM, on=on, od=od, first=True))
        for qb in range(4):
            blocks = []
            for r in range(4):
                base = r + 512 * qb
                blocks.append((ss(base, 128, 4), ss(TOK + base, 128, 4), ss(TOK + base - 512, 128, 4),
                               vd4[s][:, r, 4 + qb, :], vd4[s][:, r, 3 + qb, :], [c.B(f"vd4_{s}")]))
            on = numer[:, 512 * qb:512 * qb + 512].rearrange("p (i r) -> p r i", r=4)
            od = den[:, 512 * qb:512 * qb + 512].rearrange("p (i r) -> p r i", r=4)
            gl.append(dict(blocks=blocks, pmask=M_ALL0 if qb == 0 else M_NORM, on=on, od=od, first=False))
        for a in range(4):
            blocks = []
            for bi in range(4):
                r = 4 * a + bi
                blocks.append((ss(r, 128, 16), ss(TOK + r, 128, 16), ss(r, 128, 16),
                               vd16[s][:, r, 1, :], vd16[s][:, r, 0, :], [c.B(f"vd16_{s}")]))
            on = numer[:].rearrange("p (i r) -> p r i", r=16)[:, 4 * a:4 * a + 4, :]
            od = den[:].rearrange("p (i r) -> p r i", r=16)[:, 4 * a:4 * a + 4, :]
            gl.append(dict(blocks=blocks, pmask=M_ALL0, on=on, od=od, first=False))
        for k, g in enumerate(gl):
            g["s"], g["h"], g["last"] = s, h, (k == len(gl) - 1)
        return gl

    def head_finish(h):
        Bn, Bd, Bo = c.B("numer"), c.B("den"), c.B("obf")
        P.op(P.dve, lambda h_: h_.reciprocal(den[:], den[:]), reads=[Bd], writes=[Bd])
        P.op(P.dve, lambda h_: h_.tensor_tensor(obf[:], numer[:], den[:], ALU.mult), reads=[Bn, Bd], writes=[Bo])
        P.dma(P.sp, OT[h], obf[:], c.B("dram_OT"), Bo)
        if h + 2 < KC:
            head_load(h + 2)

    def attention(v, wb):
        head_load(0)
        head_load(1)
        allg = []
        for h in range(KC):
            allg.extend(head_groups(h))
        for i, g in enumerate(allg):
            g["gi"] = i % 2
        for i in range(len(allg) + 1):
            if i < len(allg):
                g_qk(allg[i])
            if i >= 1:
                g_pv(allg[i - 1])
                if allg[i - 1]["last"]:
                    head_finish(allg[i - 1]["h"])

    h32 = P.sbuf("h32", [128, KC, T], F32)
    ob = P.sbuf("ob", [128, KC, T], BF16)
    hbo = P.sbuf("hbo", [128, KC, T], BF16)
    h32b = [c.B(f"h32_{i}") for i in range(KC)]
    obb = [c.B(f"ob_{i}") for i in range(KC)]
    hbob = [c.B(f"hbo_{i}") for i in range(KC)]

    def load_c1(q):
        def f(v, wb):
            P.dma(P.sp, h32[:], H1T.rearrange("(c p) t -> p c t", p=128)[:, :, T * q:T * q + T], h32b, c.B("dram_H1T"))
            P.dma(P.sp, ob[:], OT.rearrange("h p t -> p h t")[:, :, T * q:T * q + T], obb, c.B("dram_OT"))
        return f

    def wo(q, oc):
        def f(W, wb):
            ps, pbk = c.main_bank()
            c.mm(ps[:, 0:T], pbk, lambda k: W[:, k, :], lambda k: ob[:, k, :], KC, wb + obb)
            c.resid_acc(oc, ps, pbk, h32, h32b)
        return f

    def fin(q):
        def f(v, wb):
            c.ln_finalize(h32, h32b, hbo, hbob, V_G1, V_B1)
            P.dma(P.sp, H2_32.rearrange("(c p) t -> p c t", p=128)[:, :, T * q:T * q + T], h32[:], c.B("dram_H2_32"), h32b)
            P.dma(P.sp, H2bH.rearrange("(c p) t -> p c t", p=128)[:, :, 2 + T * q:2 + T * q + T], hbo[:], c.B("dram_H2bH"), hbob)
            if q == NPASS - 1:
                P.dma(P.sp, halo_s.rearrange("p (c t) -> p c t", t=2), hbo[:, :, T - 2:T], c.B("dram_halo1s"), hbob)
        return f

    tasks = []
    tasks.append(("none", None, attention))
    for q in range(NPASS):
        tasks.append(("none", None, load_c1(q)))
        for oc in range(KC):
            tasks.append(("s128", w_o[:, 128 * oc:128 * oc + 128], wo(q, oc)))
        tasks.append(("none", None, fin(q)))
    c.run_tasks(tasks)
    c.end_phase()


RG = [[0, 1], [2, 3], [4, 5], [6, 7]]


def build_fused():
    c = Ctx()
    nc, P = c.nc, c.P
    io = {}
    for name, shape in [("xT", [D, HL + TOK]), ("w_in", [D, D]), ("w_grp", [4, 512, 512]), ("w_out", [D, D]),
                        ("w_q", [D, D]), ("w_k", [D, D]), ("w_v", [D, D]), ("w_o", [D, D]),
                        ("w_up0", [D, 2 * FF]), ("w_dn0", [FF, D]), ("w_up1", [D, 2 * FF]), ("w_dn1", [FF, D]),
                        ("vec0", [128, NV]), ("vec1", [128, NV]), ("invc", [128, 64]), ("hv", [128, 1])]:
        io[name] = c.din(name, shape, F32)
    io["masks"] = c.din("masks", [128, 4 * 512], BF16)
    io["out"] = c.dout("out", [D, TOK], F32)

    def di(name, shape, dt):
        return nc.dram_tensor(name, list(shape), dt).ap()
    io["Hm32"] = di("Hm32", [D, TOK], F32)
    io["HmbH"] = di("HmbH", [D, 2 + TOK], BF16)
    io["H1T"] = di("H1T", [D, TOK], F32)
    io["H2_32"] = di("H2_32", [D, TOK], F32)
    io["H2bH"] = di("H2bH", [D, 2 + TOK], BF16)
    for k in ("halo0s", "halo1s"):
        io[k] = di(k, [128, 32], BF16)
    for k in ("halo0g", "halo1g"):
        io[k] = di(k, [256, 32], BF16)
    io["QT"] = di("QT", [KC, 128, TOK], BF16)
    io["KTc"] = [di(f"KTc{i}", [128, TOK], BF16) for i in range(16)]
    io["KTg"] = [di(f"KTg{i}", [256, TOK], BF16) for i in range(16)]
    io["Vc"] = [di(f"Vc{i}", [128, D], BF16) for i in range(16)]
    io["Vgc"] = [di(f"Vgc{i}", [256, D], BF16) for i in range(16)]
    io["V2"] = di("V2", [2 * TOK, D], BF16)
    io["OT"] = di("OT", [KC, 128, TOK], BF16)

    for nm in ("dram_halo0g", "dram_Vgc", "dram_KTg", "dram_halo1g"):
        b = c.B(nm)
        b.sem = P.new_sem("c_" + nm)
        P.dma_bufs.append(b)
    load_vecs(c, io["vec0"])
    P.dma(P.sp, c.hvt[:], io["hv"], c.Bhv, c.B("dram_hv"))
    phase_A1(c, io)
    P.cc(io["halo0s"], io["halo0g"], c.B("dram_halo0g"), c.B("dram_halo0s"), RG)
    phase_FFN(c, io, 0, True)
    for i in range(16):
        P.cc(io["Vc"][i], io["Vgc"][i], c.B("dram_Vgc"), c.B("dram_Vc"), RG, kind=1)
    for i in range(16):
        P.cc(io["KTc"][i], io["KTg"][i], c.B("dram_KTg"), c.B("dram_KTc"), RG, kind=2)
    for i in range(16):
        P.dma(P.sp, io["V2"][TOK + 128 * i:TOK + 128 * i + 128, :], io["Vc"][i], c.B("dram_V2"), c.B("dram_Vc"), parallel=True)
        P.dma(P.sp, io["V2"][128 * i:128 * i + 128, :], io["Vgc"][i][0:128, :], c.B("dram_V2"), c.B("dram_Vgc"), parallel=True)
    load_vecs(c, io["vec1"])
    phase_B(c, io)
    P.cc(io["halo1s"], io["halo1g"], c.B("dram_halo1g"), c.B("dram_halo1s"), RG, kind=3)
    phase_FFN(c, io, 1, False)
    return c.finish()


def build_part(part):
    c = Ctx()
    nc, P = c.nc, c.P
    io = {}

    def I(name, shape, dt=F32):
        io[name] = c.din(name, shape, dt)

    def O(name, shape, dt=F32):
        io[name] = c.dout(name, shape, dt)
    if part == "A1":
        I("xT", [D, HL + TOK]); I("w_in", [D, D]); I("w_grp", [4, 512, 512]); I("w_out", [D, D]); I("vec0", [128, NV]); I("invc", [128, 64])
        O("Hm32", [D, TOK]); O("HmbH", [D, 2 + TOK], BF16); O("halo0s", [128, 32], BF16)
        load_vecs(c, io["vec0"])
        phase_A1(c, io)
    elif part in ("F0", "F1"):
        L = int(part[1])
        a32, abH, ahg = ("Hm32", "HmbH", "halo0g") if L == 0 else ("H2_32", "H2bH", "halo1g")
        I(a32, [D, TOK]); I(abH, [D, 2 + TOK], BF16); I(ahg, [256, 32], BF16); I("hv", [128, 1])
        I(f"w_up{L}", [D, 2 * FF]); I(f"w_dn{L}", [FF, D]); I(f"vec{L}", [128, NV])
        if L == 0:
            I("w_q", [D, D]); I("w_k", [D, D]); I("w_v", [D, D])
            O("H1T", [D, TOK]); O("QT", [KC, 128, TOK], BF16); O("KTall", [D, TOK], BF16); O("Vall", [TOK, D], BF16)
            io["KTc"] = [io["KTall"][128 * i:128 * i + 128, :] for i in range(16)]
            io["Vc"] = [io["Vall"][128 * i:128 * i + 128, :] for i in range(16)]
            c.outs += [c.B("dram_KTc"), c.B("dram_Vc")]
        else:
            O("out", [D, TOK])
        load_vecs(c, io[f"vec{L}"])
        P.dma(P.sp, c.hvt[:], io["hv"], c.Bhv, c.B("dram_hv"))
        phase_FFN(c, io, L, L == 0)
    else:
        I("QT", [KC, 128, TOK], BF16); I("KTall", [D, TOK], BF16); I("KTgall", [16 * 256, TOK], BF16); I("V2", [2 * TOK, D], BF16)
        I("H1T", [D, TOK]); I("w_o", [D, D]); I("vec1", [128, NV]); I("masks", [128, 4 * 512], BF16)
        O("H2_32", [D, TOK]); O("H2bH", [D, 2 + TOK], BF16); O("halo1s", [128, 32], BF16)
        io["KTc"] = [io["KTall"][128 * i:128 * i + 128, :] for i in range(16)]
        io["KTg"] = [io["KTgall"][256 * i:256 * i + 256, :] for i in range(16)]
        io["OT"] = nc.dram_tensor("OT", [KC, 128, TOK], BF16).ap()
        load_vecs(c, io["vec1"])
        phase_B(c, io)
    return c.finish()


def _cols(v, n):
    return np.ascontiguousarray(np.asarray(v, np.float32).reshape(n, 128).T)


def _pack_vecs(scale, g1, b1, g2, b2, cw, cb):
    out = np.zeros((128, NV), np.float32)
    if scale is not None:
        out[:, V_SCALE:V_SCALE + 16] = _cols(scale, 16)
    out[:, V_G1:V_G1 + 16] = _cols(g1, 16)
    out[:, V_B1:V_B1 + 16] = _cols(b1, 16)
    out[:, V_G2:V_G2 + 16] = _cols(g2, 16)
    out[:, V_B2:V_B2 + 16] = _cols(b2, 16)
    out[:, V_CW0:V_CW0 + 86] = _cols(cw[0], 86)
    out[:, V_CW1:V_CW1 + 86] = _cols(cw[1], 86)
    out[:, V_CW2:V_CW2 + 86] = _cols(cw[2], 86)
    out[:, V_CB:V_CB + 86] = _cols(cb, 86)
    return out


_NC_CACHE = {}
DEBUG = None


def _get(name, fn):
    if name not in _NC_CACHE:
        _NC_CACHE[name] = fn()
    return _NC_CACHE[name]


def _run(nc, in_maps):
    res = run_bass_kernel_spmd(nc, in_maps, core_ids=list(range(NCORES)))
    return res.results


def _masks(half):
    kj = np.arange(128)[:, None]
    qi = np.arange(128)[None, :]
    mp = (kj >= qi).astype(np.float32)
    mc = (kj <= qi).astype(np.float32)
    p0 = mp if half == 1 else np.zeros_like(mp)
    norm = np.concatenate([mp] * 4, 1)
    first1 = np.concatenate([p0, mp, mp, mp], 1)
    all0 = np.concatenate([p0] * 4, 1)
    cur = np.concatenate([mc] * 4, 1)
    return np.concatenate([norm, first1, all0, cur], 1).astype(ml_dtypes.bfloat16)


def _invc(half):
    out = np.zeros((128, 4, 16), np.float32)
    for g in range(4):
        w = 2 << g
        t = np.arange(16)
        if half == 0:
            out[:, g, :] = 1.0 / np.minimum(t + 1, w)
        else:
            out[:, g, :] = 1.0 / w
    return out.reshape(128, 64)


def kernel(x, pool_w_in, pool_w_grp, pool_scale, pool_w_out, attn_w_q, attn_w_o,
           shared_w_k, shared_w_v, ffn_w_up, ffn_conv_w, ffn_conv_b, ffn_w_down,
           ln1_g, ln1_b, ln2_g, ln2_b):
    f = lambda a: np.ascontiguousarray(np.asarray(a, np.float32))
    x = f(x)
    bf = ml_dtypes.bfloat16
    vec0 = _pack_vecs(pool_scale[0], ln1_g[0], ln1_b[0], ln2_g[0], ln2_b[0], np.asarray(ffn_conv_w[0]), ffn_conv_b[0])
    vec1 = _pack_vecs(None, ln1_g[1], ln1_b[1], ln2_g[1], ln2_b[1], np.asarray(ffn_conv_w[1]), ffn_conv_b[1])
    w_in, w_grp, w_out = f(pool_w_in[0]), f(pool_w_grp[0]), f(pool_w_out[0])
    w_q, w_o, w_k, w_v = f(attn_w_q[0]), f(attn_w_o[0]), f(shared_w_k), f(shared_w_v)
    w_up0, w_up1 = f(ffn_w_up[0]), f(ffn_w_up[1])
    w_dn0, w_dn1 = f(ffn_w_down[0]), f(ffn_w_down[1])
    cores = [(b, hf) for b in range(4) for hf in range(2)]
    xts = []
    for (b, hf) in cores:
        xt = np.zeros((D, HL + TOK), np.float32)
        lo = hf * TOK
        xt[:, HL:] = x[b, lo:lo + TOK, :].T
        if hf == 1:
            xt[:, :HL] = x[b, lo - HL:lo, :].T
        xts.append(xt)
    hvs = [np.full((128, 1), float(hf), np.float32) for (b, hf) in cores]
    if FUSED:
        maps = [dict(xT=xts[i], w_in=w_in, w_grp=w_grp, w_out=w_out, w_q=w_q, w_k=w_k, w_v=w_v, w_o=w_o,
                     w_up0=w_up0, w_dn0=w_dn0, w_up1=w_up1, w_dn1=w_dn1, vec0=vec0, vec1=vec1,
                     invc=_invc(hf), hv=hvs[i], masks=_masks(hf)) for i, (b, hf) in enumerate(cores)]
        res = _run(_get("fused", build_fused), maps)
    else:
        def halo(prev, key):
            outs = []
            for i, (b, hf) in enumerate(cores):
                h = np.zeros((256, 32), bf)
                if hf == 1:
                    h[0:128] = prev[i - 1][key]
                outs.append(h)
            return outs
        r1 = _run(_get("A1", lambda: build_part("A1")),
                  [dict(xT=xts[i], w_in=w_in, w_grp=w_grp, w_out=w_out, vec0=vec0, invc=_invc(hf)) for i, (b, hf) in enumerate(cores)])
        hg = halo(r1, "halo0s")
        r2 = _run(_get("F0", lambda: build_part("F0")),
                  [dict(Hm32=r1[i]["Hm32"], HmbH=r1[i]["HmbH"], halo0g=hg[i], hv=hvs[i], w_up0=w_up0, w_dn0=w_dn0, vec0=vec0,
                        w_q=w_q, w_k=w_k, w_v=w_v) for i in range(NCORES)])
        maps = []
        for i, (b, hf) in enumerate(cores):
            ktg = np.zeros((16, 256, TOK), bf)
            vprev = np.zeros((TOK, D), bf)
            if hf == 1:
                ktg[:, 0:128, :] = np.asarray(r2[i - 1]["KTall"]).reshape(16, 128, TOK)
                vprev = r2[i - 1]["Vall"]
            maps.append(dict(QT=r2[i]["QT"], KTall=r2[i]["KTall"], KTgall=ktg.reshape(16 * 256, TOK),
                             V2=np.concatenate([vprev, r2[i]["Vall"]], 0), H1T=r2[i]["H1T"], w_o=w_o, vec1=vec1, masks=_masks(hf)))
        r3 = _run(_get("B", lambda: build_part("B")), maps)
        hg = halo(r3, "halo1s")
        res = _run(_get("F1", lambda: build_part("F1")),
                   [dict(H2_32=r3[i]["H2_32"], H2bH=r3[i]["H2bH"], halo1g=hg[i], hv=hvs[i], w_up1=w_up1, w_dn1=w_dn1, vec1=vec1)
                    for i in range(NCORES)])
    out = np.zeros((4, 2 * TOK, D), np.float32)
    for i, (b, hf) in enumerate(cores):
        out[b, hf * TOK:(hf + 1) * TOK, :] = res[i]["out"].T
    return out
```
